# Optimizing a Trainium2 kernel written in Bass

```python
import math
import jax, jax.numpy as jnp
from jax import lax
import numpy as np

D_MODEL = 2048
BATCH = 1
SEQ = 8192
DEPTH = 1

N_META = 16
ATTN_WIDTH = D_MODEL // 2
CONV_WIDTH = D_MODEL - ATTN_WIDTH
HEAD_DIM = 128
N_ATTN_HEADS = ATTN_WIDTH // HEAD_DIM
CONV_K = 3
D_FF = -(-(8 * D_MODEL) // (3 * 256)) * 256
Q_BLOCK = 128
IN_COLS = 3 * ATTN_WIDTH + 3 * CONV_WIDTH
EPS = 1e-6

kernel_name = "hymba_stickbreak_shortconv_swiglu"


def _rmsnorm(x, g):
    xf = x.astype(jnp.float32)
    y = xf * lax.rsqrt(jnp.mean(xf * xf, axis=-1, keepdims=True) + EPS)
    return (y * g.astype(jnp.float32)).astype(x.dtype)


def _stick_breaking_block(q_blk, k_ctx, v_ctx, q_start):
    nq = q_blk.shape[1]
    nk = k_ctx.shape[1]
    z = jnp.einsum('bqhd,bkhd->bhqk', q_blk.astype(jnp.float32),
                   k_ctx.astype(jnp.float32)) / math.sqrt(HEAD_DIM)
    qpos = q_start + jnp.arange(nq)
    kpos = jnp.arange(nk)
    causal = kpos[None, :] < qpos[:, None]
    log_beta = jax.nn.log_sigmoid(z)
    log_keep = jnp.where(causal, jax.nn.log_sigmoid(-z), 0.0)
    log_pass = lax.cumsum(log_keep, axis=3, reverse=True) - log_keep
    w = jnp.where(causal, jnp.exp(log_beta + log_pass), 0.0)
    o = jnp.einsum('bhqk,bkhd->bqhd', w, v_ctx.astype(jnp.float32))
    return o.astype(q_blk.dtype)


def _stick_breaking_attention(q, k, v):
    bounds = [(0, N_META)] + [(N_META + i * Q_BLOCK, N_META + (i + 1) * Q_BLOCK)
                              for i in range(SEQ // Q_BLOCK)]
    outs = [_stick_breaking_block(q[:, s:e], k[:, :e], v[:, :e], s) for s, e in bounds]
    return jnp.concatenate(outs, axis=1)


def _causal_dwconv(u, w):
    rhs = w[:, None, :].astype(u.dtype)
    return lax.conv_general_dilated(
        u, rhs, window_strides=(1,), padding=[(CONV_K - 1, 0)],
        dimension_numbers=('NWC', 'WIO', 'NWC'), feature_group_count=u.shape[-1])


def setup_inputs(seed: int = 0) -> dict:
    key = jax.random.key(seed)
    ks = jax.random.split(key, 16)
    f32 = jnp.float32

    def gain(k, shape):
        return 1.0 + 0.02 * jax.random.normal(k, shape, f32)

    return {
        "x": jax.random.normal(ks[0], (BATCH, SEQ, D_MODEL), f32),
        "meta_tokens": jax.random.normal(ks[1], (N_META, D_MODEL), f32),
        "g_mix": gain(ks[2], (DEPTH, D_MODEL)),
        "w_in": jax.random.normal(ks[3], (DEPTH, D_MODEL, IN_COLS), f32) * D_MODEL ** -0.5,
        "g_q": gain(ks[4], (DEPTH, HEAD_DIM)),
        "g_k": gain(ks[5], (DEPTH, HEAD_DIM)),
        "conv_w": jax.random.normal(ks[6], (DEPTH, CONV_K, CONV_WIDTH), f32) * CONV_K ** -0.5,
        "g_attn_out": gain(ks[7], (DEPTH, ATTN_WIDTH)),
        "g_conv_out": gain(ks[8], (DEPTH, CONV_WIDTH)),
        "w_out": jax.random.normal(ks[9], (DEPTH, D_MODEL, D_MODEL), f32) * D_MODEL ** -0.5,
        "g_ffn": gain(ks[10], (DEPTH, D_MODEL)),
        "w_gate": jax.random.normal(ks[11], (DEPTH, D_MODEL, D_FF), f32) * D_MODEL ** -0.5,
        "w_up": jax.random.normal(ks[12], (DEPTH, D_MODEL, D_FF), f32) * D_MODEL ** -0.5,
        "w_down": jax.random.normal(ks[13], (DEPTH, D_FF, D_MODEL), f32) * D_FF ** -0.5,
    }


def reference(x, meta_tokens, g_mix, w_in, g_q, g_k, conv_w, g_attn_out, g_conv_out,
              w_out, g_ffn, w_gate, w_up, w_down):
    b = x.shape[0]
    meta = jnp.broadcast_to(meta_tokens.astype(x.dtype)[None], (b, N_META, D_MODEL))
    h = jnp.concatenate([meta, x], axis=1)
    L = h.shape[1]

    for l in range(DEPTH):
        n = _rmsnorm(h, g_mix[l])
        p = n @ w_in[l]
        q, k, v, gb, gc, u = jnp.split(
            p, np.cumsum([ATTN_WIDTH, ATTN_WIDTH, ATTN_WIDTH, CONV_WIDTH, CONV_WIDTH]), axis=-1)

        q = _rmsnorm(q.reshape(b, L, N_ATTN_HEADS, HEAD_DIM), g_q[l])
        k = _rmsnorm(k.reshape(b, L, N_ATTN_HEADS, HEAD_DIM), g_k[l])
        v = v.reshape(b, L, N_ATTN_HEADS, HEAD_DIM)
        o_attn = _stick_breaking_attention(q, k, v).reshape(b, L, ATTN_WIDTH)

        o_conv = gb * _causal_dwconv(gc * u, conv_w[l])

        o = jnp.concatenate([_rmsnorm(o_attn, g_attn_out[l]),
                             _rmsnorm(o_conv, g_conv_out[l])], axis=-1)
        h = h + o @ w_out[l]

        n2 = _rmsnorm(h, g_ffn[l])
        h = h + (jax.nn.silu(n2 @ w_gate[l]) * (n2 @ w_up[l])) @ w_down[l]

    return h[:, N_META:]
```

```python
import math
import bisect
import contextlib
import numpy as np
import concourse.bass as bass
import concourse.mybir as mybir
from concourse.bass_utils import run_bass_kernel_spmd

F32 = mybir.dt.float32
BF16 = mybir.dt.bfloat16
AF = mybir.ActivationFunctionType
ALU = mybir.AluOpType

NCORES = 8
D = 2048
SEQ = 8192
NMETA = 16
L = SEQ + NMETA
DFF = 5632
NF = DFF // 128
KC = D // 128
EPS = 1e-6
BIG = 30000.0
G1 = 256
NG1 = SEQ // G1
GQ = 512
NGQ = SEQ // GQ
TOK3 = SEQ // NCORES
NQ = 4
FQ = NF // NQ


class Buf:
    __slots__ = ("w", "r")

    def __init__(self):
        self.w = None
        self.r = {}


class Dyn:
    def __init__(self, f):
        self.f = f


class Prog:
    ENG = ("pe", "act", "dve", "pool", "sp")

    def __init__(self):
        self.q = {e: [] for e in self.ENG}
        self.nops = {e: 0 for e in self.ENG}
        self.needed = {e: set() for e in self.ENG}
        self.dma_cnt = {}

    def op(self, eng, meth, kw=None, reads=(), writes=(), dma=None, dma_inc=16, extra=()):
        waits = {}

        def add(tok):
            if tok is None:
                return
            s, v = tok
            if s == "e_pe" and eng == "pe":
                return
            if waits.get(s, 0) < v:
                waits[s] = v

        for b in reads:
            add(b.w)
        for b in writes:
            add(b.w)
            for s, v in b.r.items():
                add((s, v))
        for t in extra:
            add(t)
        for s, v in waits.items():
            if s.startswith("e_"):
                self.needed[s[2:]].add(v)
        tok = None
        if meth is None:
            self.q[eng].append((waits, None, None, None))
        elif dma is not None:
            self.dma_cnt[dma] = self.dma_cnt.get(dma, 0) + dma_inc
            tok = (dma, self.dma_cnt[dma])
            self.q[eng].append((waits, meth, kw, ("dma", dma, dma_inc)))
        else:
            self.nops[eng] += 1
            tok = ("e_" + eng, self.nops[eng])
            self.q[eng].append((waits, meth, kw, ("eng", self.nops[eng])))
        if tok is not None:
            for b in writes:
                b.w = tok
                b.r = {}
            for b in reads:
                s, v = tok
                if b.r.get(s, 0) < v:
                    b.r[s] = v
        return tok

    def barrier_tokens(self):
        return [("e_" + e, self.nops[e]) for e in self.ENG if self.nops[e] > 0] + \
               [(s, v) for s, v in self.dma_cnt.items()]

    def mm(self, out, lhsT, rhs, start, stop, reads, writes):
        return self.op("pe", "matmul", dict(out=out, lhsT=lhsT, rhs=rhs, start=start, stop=stop), reads, writes)

    def act(self, out, in_, func, reads, writes, bias=None, scale=None):
        kw = dict(out=out, in_=in_, func=func)
        if bias is not None:
            kw["bias"] = bias
        if scale is not None:
            kw["scale"] = scale
        return self.op("act", "activation", kw, reads, writes)

    def tt(self, eng, out, in0, in1, op, reads, writes):
        return self.op(eng, "tensor_tensor", dict(out=out, in0=in0, in1=in1, op=op), reads, writes)

    def stt(self, out, in0, scalar, in1, op0, op1, reads, writes):
        return self.op("dve", "scalar_tensor_tensor", dict(out=out, in0=in0, scalar=scalar, in1=in1, op0=op0, op1=op1),
                       reads, writes)

    def ts(self, eng, out, in0, s1, op0, reads, writes, s2=None, op1=None):
        kw = dict(out=out, in0=in0, scalar1=s1, scalar2=s2, op0=op0)
        if op1 is not None:
            kw["op1"] = op1
        return self.op(eng, "tensor_scalar", kw, reads, writes)

    def copy(self, eng, out, in_, reads, writes):
        return self.op(eng, "tensor_copy", dict(out=out, in_=in_), reads, writes)

    def dma(self, eng, out, in_, sem, reads=(), writes=(), extra=()):
        return self.op(eng, "dma_start", dict(out=out, in_=in_), reads, writes, dma=sem, extra=extra)


def build_program(stop_after=3, debug=False):
    nc = bass.Bass("TRN2", target_bir_lowering=False)
    P = Prog()
    es = contextlib.ExitStack()

    hT = nc.dram_tensor("hT", [D, L], F32, kind="ExternalInput").ap()
    win = nc.dram_tensor("win", [128, KC * 768], F32, kind="ExternalInput").ap()
    prm = nc.dram_tensor("prm", [128, 64], F32, kind="ExternalInput").ap()
    wout = nc.dram_tensor("wout", [KC, 128, KC * 128], F32, kind="ExternalInput").ap()
    wg = nc.dram_tensor("wg", [NF, 128, KC * 128], F32, kind="ExternalInput").ap()
    wu = nc.dram_tensor("wu", [NF, 128, KC * 128], F32, kind="ExternalInput").ap()
    wd = nc.dram_tensor("wd", [KC, 128, NF * 128], F32, kind="ExternalInput").ap()
    y = nc.dram_tensor("y", [D, TOK3], F32, kind="ExternalOutput").ap()
    inb = nc.dram_tensor("inb", [NCORES * 256, TOK3], BF16)
    outb = nc.dram_tensor("outb", [NCORES * NCORES * 256, TOK3], BF16)
    dbg = {}
    if debug:
        dbg["qT"] = nc.dram_tensor("d_qT", [128, SEQ], BF16, kind="ExternalOutput").ap()
        dbg["kT"] = nc.dram_tensor("d_kT", [128, L], BF16, kind="ExternalOutput").ap()
        dbg["v"] = nc.dram_tensor("d_v", [128, 65 * 128], BF16, kind="ExternalOutput").ap()
        dbg["inb"] = nc.dram_tensor("d_inb", [NCORES * 256, TOK3], BF16, kind="ExternalOutput").ap()

    hT3 = hT.rearrange("(k p) t -> p k t", p=128)

    def sb(name, shape, dt):
        return es.enter_context(nc.sbuf_tensor(name, shape, dt))

    ABFN = 57600
    AF32 = sb("arena_f32", [128, 16384], F32)
    ABF = sb("arena_bf", [128, ABFN], BF16)
    prm_sb = sb("prm_sb", [128, 64], F32)
    cst = sb("cst", [128, 5 * 128 + 5 * 512], BF16)
    gqs = sb("gqs", [128, 2], F32)

    ones_bf = cst[:, 0:128]
    ident_bf = cst[:, 128:256]
    tri_bf = cst[:, 256:384]
    nbigi = cst[:, 384:512]
    pbigi = cst[:, 512:640]
    maskB = [cst[:, 640 + 512 * i: 640 + 512 * (i + 1)] for i in range(5)]

    gmix = lambda k: prm_sb[:, k:k + 1]
    gq_ap = prm_sb[:, 16:17]
    gk_ap = prm_sb[:, 17:18]
    cw = lambda k: prm_sb[:, 18 + k:19 + k]
    gout = lambda k: prm_sb[:, 21 + k:22 + k]
    gffn = lambda k: prm_sb[:, 37 + k:38 + k]

    o = [0]

    def carve(n):
        a = o[0]
        o[0] += n
        return a

    o_win = carve(KC * 768)
    o_qT = carve(SEQ)
    o_kT = carve(8224)
    o_v = carve(65 * 128)
    o_nbuf = carve(2 * KC * G1)
    o_sq = carve(KC * G1)
    o_qsq = carve(G1)
    o_ksq = carve(G1)
    o_vT = carve(G1)
    o_oc = carve(2 * G1)
    o_sp = carve(3 * 512)
    o_w = carve(2 * 512)
    o_S = carve(2 * 512)
    o_qn = carve(2 * 512)
    o_ot = carve(2 * 512)
    assert o[0] <= ABFN, o[0]

    win_sb = ABF[:, o_win:o_win + KC * 768].rearrange("p (k n) -> p k n", k=KC)
    qT = ABF[:, o_qT:o_qT + SEQ]
    kT = ABF[:, o_kT:o_kT + L]
    v_flat = ABF[:, o_v:o_v + 65 * 128]
    v_sb = v_flat.rearrange("p (t d) -> p t d", d=128)
    nbuf = [ABF[:, o_nbuf + s * KC * G1: o_nbuf + (s + 1) * KC * G1].rearrange("p (k n) -> p k n", k=KC)
            for s in range(2)]
    sqb = [ABF[:, o_sq + i * G1:o_sq + (i + 1) * G1] for i in range(KC)]
    qsq = ABF[:, o_qsq:o_qsq + G1]
    ksq = ABF[:, o_ksq:o_ksq + G1]
    vT_bf = ABF[:, o_vT:o_vT + G1]
    oc_st = [ABF[:, o_oc + i * G1:o_oc + (i + 1) * G1] for i in range(2)]
    sp_b = [ABF[:, o_sp + i * 512:o_sp + (i + 1) * 512] for i in range(3)]
    w_b = [ABF[:, o_w + i * 512:o_w + (i + 1) * 512] for i in range(2)]
    S_b = [ABF[:, o_S + i * 512:o_S + (i + 1) * 512] for i in range(2)]
    qn_b = [ABF[:, o_qn + i * 512:o_qn + (i + 1) * 512] for i in range(2)]
    ot_st = [ABF[:, o_ot + i * 512:o_ot + (i + 1) * 512] for i in range(2)]

    NXS = 3
    xbuf = [AF32[:, s * KC * G1:(s + 1) * KC * G1].rearrange("p (k n) -> p k n", k=KC) for s in range(NXS)]
    fo = [NXS * KC * G1]

    def fcarve(n):
        a = fo[0]
        fo[0] += n
        return AF32[:, a:a + n]

    lnv = [fcarve(G1) for _ in range(2)]
    lnq = fcarve(G1)
    lnk = fcarve(G1)
    qr = fcarve(G1)
    kr = fcarve(G1)
    b_sb = fcarve(G1)
    c_sb = fcarve(G1)
    cu = fcarve(G1 + 8)
    y1 = fcarve(G1)
    y2 = fcarve(G1)
    assert fo[0] <= 16384
    e_t = sb("e_t", [128, 2 * 512], F32)
    e_b = [e_t[:, 0:512], e_t[:, 512:1024]]

    dma_sem_names = []

    def dsem(name):
        if name not in dma_sem_names:
            dma_sem_names.append(name)
        return name

    banks = [es.enter_context(nc.psum_tensor("bank%d" % i, [128, 512], F32)) for i in range(8)]
    bankB = [Buf() for _ in range(8)]

    B_prm = Buf()
    P.dma("sp", prm_sb[:], prm, dsem("d_prm"), writes=[B_prm])
    B_cst = Buf()
    P.op("pool", "memset", dict(ap=cst[:, 0:128], constant=1.0), writes=[B_cst])
    P.op("pool", "memset", dict(ap=cst[:, 640:640 + 2560], constant=1.0), writes=[B_cst])
    P.op("pool", "affine_select", dict(out=ident_bf, in_=ones_bf, pattern=[[1, 128]], compare_op=ALU.is_equal,
                                       fill=0.0, base=0, channel_multiplier=-1), writes=[B_cst])
    P.op("pool", "affine_select", dict(out=tri_bf, in_=ones_bf, pattern=[[-1, 128]], compare_op=ALU.is_ge,
                                       fill=0.0, base=0, channel_multiplier=1), writes=[B_cst])
    P.ts("pool", nbigi, ident_bf, -BIG, ALU.mult, [], [B_cst], s2=1.0, op1=ALU.mult)
    P.ts("pool", pbigi, ident_bf, BIG, ALU.mult, [], [B_cst], s2=1.0, op1=ALU.mult)
    for i in range(4):
        P.op("pool", "affine_select", dict(out=maskB[i], in_=maskB[i], pattern=[[-1, 512]], compare_op=ALU.is_ge,
                                           fill=0.0, base=128 * i, channel_multiplier=1), writes=[B_cst])
    P.op("pool", "affine_select", dict(out=maskB[4], in_=maskB[4], pattern=[[0, 512]], compare_op=ALU.is_ge,
                                       fill=0.0, base=-NMETA, channel_multiplier=1), writes=[B_cst])
    B_gqs = Buf()
    P.ts("dve", gqs[:, 0:1], gq_ap, 1.0 / math.sqrt(128.0), ALU.mult, [B_prm], [B_gqs])
    B_cu = Buf()
    P.op("dve", "memset", dict(ap=cu[:, 0:G1 + 8], constant=0.0), writes=[B_cu])

    B_win = [Buf() for _ in range(4)]
    win3 = win.rearrange("p (k n) -> p k n", k=KC)
    for i in range(4):
        P.dma("pool", win_sb[:, 4 * i:4 * i + 4, :], win3[:, 4 * i:4 * i + 4, :], dsem("d_win%d" % i), writes=[B_win[i]])

    B_x = [[Buf() for _ in range(4)] for _ in range(NXS)]
    B_sq = [Buf() for _ in range(KC)]
    B_n = [[Buf() for _ in range(KC)] for _ in range(2)]
    B_lnv = [Buf(), Buf()]
    B_lnq, B_lnk, B_qr, B_kr, B_qsq, B_ksq, B_vT, B_b, B_c, B_y1, B_y2 = [Buf() for _ in range(11)]
    B_oc = [Buf(), Buf()]
    NGRP = NG1 + 1
    B_qT = [Buf() for _ in range(NGRP)]
    B_kT = [Buf() for _ in range(NGRP)]
    B_v = [Buf() for _ in range(NGRP)]
    P.op("pool", "memset", dict(ap=v_sb[:, 0, :], constant=0.0), writes=[B_v[0]])
    acc_rr = [0]
    store_toks = [[] for _ in range(NCORES)]

    def grp_info(gi):
        if gi == 0:
            return NMETA, 0
        return G1, NMETA + G1 * (gi - 1)

    def dmaX(gi):
        N, c0 = grp_info(gi)
        s = gi % NXS
        for i in range(4):
            P.dma("sp", xbuf[s][:, 4 * i:4 * i + 4, 0:N], hT3[:, 4 * i:4 * i + 4, c0:c0 + N], dsem("d_x%d_%d" % (s, i)),
                  writes=[B_x[s][i]])

    def squares(gi):
        N, c0 = grp_info(gi)
        s = gi % NXS
        for k in range(KC):
            P.act(sqb[k][:, 0:N], xbuf[s][:, k, 0:N], AF.Square, [B_x[s][k // 4]], [B_sq[k]])

    def stageA2(gi):
        N, c0 = grp_info(gi)
        s = gi % NXS
        n = gi % 2
        rb = gi % 2
        for k in range(KC):
            P.mm(banks[rb][:, 0:N], ones_bf, sqb[k][:, 0:N], k == 0, k == KC - 1, [B_sq[k], B_cst], [bankB[rb]])
        P.act(lnv[n][:, 0:N], banks[rb][:, 0:N], AF.Ln, [bankB[rb]], [B_lnv[n]], bias=EPS, scale=1.0 / D)
        P.act(banks[rb][:, 0:N], lnv[n][:, 0:N], AF.Exp, [B_lnv[n]], [bankB[rb]], scale=-0.5)
        for k in range(KC):
            P.stt(nbuf[n][:, k, 0:N], xbuf[s][:, k, 0:N], gmix(k), banks[rb][:, 0:N], ALU.mult, ALU.mult,
                  [B_x[s][k // 4], bankB[rb], B_prm], [B_n[n][k]])

    def stageB(gi, js):
        N, c0 = grp_info(gi)
        s = gi % 2
        is_meta = gi == 0
        xc0 = c0 - NMETA
        for j in js:
            if is_meta and j in (0, 3):
                continue
            a = 2 + acc_rr[0] % 3
            acc_rr[0] += 1
            for k in range(KC):
                P.mm(banks[a][:, 0:N], win_sb[:, k, j * 128:(j + 1) * 128], nbuf[s][:, k, 0:N], k == 0, k == KC - 1,
                     [B_n[s][k], B_win[k // 4]], [bankB[a]])
            accv = banks[a][:, 0:N]
            if j in (0, 1):
                raw, B_raw, sqt, B_sqt, lnt, B_lnt = (qr, B_qr, qsq, B_qsq, lnq, B_lnq) if j == 0 else \
                    (kr, B_kr, ksq, B_ksq, lnk, B_lnk)
                m = 5 + j
                P.act(raw[:, 0:N], accv, AF.Copy, [bankB[a]], [B_raw])
                P.act(sqt[:, 0:N], accv, AF.Square, [bankB[a]], [B_sqt])
                P.mm(banks[m][:, 0:N], ones_bf, sqt[:, 0:N], True, True, [B_sqt, B_cst], [bankB[m]])
                P.act(lnt[:, 0:N], banks[m][:, 0:N], AF.Ln, [bankB[m]], [B_lnt], bias=EPS, scale=1.0 / 128)
                P.act(banks[m][:, 0:N], lnt[:, 0:N], AF.Exp, [B_lnt], [bankB[m]], scale=-0.5)
                if j == 0:
                    P.stt(qT[:, xc0:xc0 + N], raw[:, 0:N], gqs[:, 0:1], banks[m][:, 0:N], ALU.mult, ALU.mult,
                          [B_raw, bankB[m], B_gqs], [B_qT[gi]])
                else:
                    P.stt(kT[:, c0:c0 + N], raw[:, 0:N], gk_ap, banks[m][:, 0:N], ALU.mult, ALU.mult,
                          [B_raw, bankB[m], B_prm], [B_kT[gi]])
            elif j == 2:
                P.act(vT_bf[:, 0:N], accv, AF.Copy, [bankB[a]], [B_vT])
                m = 5
                if is_meta:
                    P.mm(banks[m][0:NMETA, 0:128], vT_bf[:, 0:NMETA], ident_bf, True, True, [B_vT, B_cst], [bankB[m]])
                    P.copy("dve", v_sb[0:NMETA, 0, :], banks[m][0:NMETA, 0:128], [bankB[m]], [B_v[gi]])
                else:
                    nb = N // 128
                    for i in range(nb):
                        P.mm(banks[m][:, 128 * i:128 * (i + 1)], vT_bf[:, 128 * i:128 * (i + 1)], ident_bf, True, True,
                             [B_vT, B_cst], [bankB[m]])
                    t0 = 1 + xc0 // 128
                    P.copy("dve", v_sb[:, t0:t0 + nb, :], banks[m][:, 0:N].rearrange("p (t d) -> p t d", d=128),
                           [bankB[m]], [B_v[gi]])
            elif j == 3:
                P.act(b_sb[:, 0:N], accv, AF.Copy, [bankB[a]], [B_b])
            elif j == 4:
                P.act(c_sb[:, 0:N], accv, AF.Copy, [bankB[a]], [B_c])
            else:
                P.tt("dve", cu[:, 2:2 + N], accv, c_sb[:, 0:N], ALU.mult, [bankB[a], B_c], [B_cu])
                if not is_meta:
                    so = (gi - 1) % 2
                    P.ts("dve", y1[:, 0:N], cu[:, 2:2 + N], cw(2), ALU.mult, [B_cu, B_prm], [B_y1])
                    P.stt(y2[:, 0:N], cu[:, 1:1 + N], cw(1), y1[:, 0:N], ALU.mult, ALU.add, [B_cu, B_y1, B_prm], [B_y2])
                    P.stt(y1[:, 0:N], cu[:, 0:N], cw(0), y2[:, 0:N], ALU.mult, ALU.add, [B_cu, B_y2, B_prm], [B_y1])
                    P.tt("dve", oc_st[so][:, 0:N], y1[:, 0:N], b_sb[:, 0:N], ALU.mult, [B_y1, B_b], [B_oc[so]])
                    pc, cc0 = xc0 // TOK3, xc0 % TOK3
                    store_toks[pc].append(P.dma("pool", inb.ap()[256 * pc + 128:256 * pc + 256, cc0:cc0 + N], oc_st[so][:, 0:N],
                                                dsem("d_oc%d" % so), reads=[B_oc[so]]))
                P.copy("dve", cu[:, 0:2], cu[:, N:N + 2], [B_cu], [B_cu])

    for gi in range(NXS):
        dmaX(gi)
    squares(0)
    stageA2(0)
    squares(1)
    for gi in range(NGRP):
        if gi + NXS < NGRP:
            dmaX(gi + NXS)
        stageB(gi, [0, 1, 2])
        if gi + 1 < NGRP:
            stageA2(gi + 1)
        stageB(gi, [3, 4, 5])
        if gi + 2 < NGRP:
            squares(gi + 2)
    bar1 = P.barrier_tokens()

    if debug:
        P.dma("sp", dbg["qT"], qT, dsem("d_dbg"), reads=B_qT)
        P.dma("sp", dbg["kT"], kT, dsem("d_dbg"), reads=B_kT)
        P.dma("sp", dbg["v"], v_flat, dsem("d_dbg"), reads=B_v)

    hbuf = AF32[:, 0:KC * TOK3].rearrange("p (k n) -> p k n", k=KC)
    B_h = [Buf() for _ in range(KC)]
    if stop_after >= 3:
        for i in range(8):
            P.op("sp", "dma_start",
                 dict(out=hbuf[:, 2 * i:2 * i + 2, :],
                      in_=Dyn((lambda i: lambda ctx: hT3[:, 2 * i:2 * i + 2, bass.ds(ctx["pid"] * TOK3 + NMETA, TOK3)])(i))),
                 writes=[B_h[2 * i], B_h[2 * i + 1]], dma=dsem("d_h%d" % i), extra=bar1 if i == 0 else ())

    if stop_after >= 2:
        Zb, Pb, Ob = [0, 1], [2, 3], [4, 7]
        B_e = [Buf(), Buf()]
        B_sp = [Buf(), Buf(), Buf()]
        B_w = [Buf(), Buf()]
        B_S = Buf()
        B_qn = [Buf(), Buf()]
        B_ot = [Buf(), Buf()]
        S_t = S_b[0]
        units = []
        for g in range(NGQ):
            tiles = list(range(4 * g + 3, -1, -1)) + [-1]
            for idx, J in enumerate(tiles):
                units.append(dict(g=g, J=J, first=(idx == 0), last=(idx == len(tiles) - 1), u=len(units)))

        def kinfo(J):
            if J < 0:
                return 0, 0, 0
            return NMETA + 128 * J, 1 + J, 1 + (128 * J) // G1

        def k_bufs(J):
            if J < 0:
                return [B_kT[0], B_kT[1]]
            return [B_kT[1 + (128 * J) // G1]]

        def mask_of(g, J):
            if J < 0:
                return maskB[4]
            return maskB[J - 4 * g] if J >= 4 * g else None

        def col0(g, J):
            if J >= 4 * g:
                return 128 * (J - 4 * g)
            return 0

        def q_bufs(g):
            return [B_qT[1 + (GQ * g) // G1 + i] for i in range(GQ // G1)]

        def S1(un):
            g, J, u = un["g"], un["J"], un["u"]
            kc, vt, kb = kinfo(J)
            z = Zb[u % 2]
            mk = mask_of(g, J)
            c0 = col0(g, J)
            qv = qT[:, GQ * g + c0:GQ * (g + 1)]
            P.mm(banks[z][:, c0:512], kT[:, kc:kc + 128], qv, True, mk is None, k_bufs(J) + q_bufs(g), [bankB[z]])
            if mk is not None:
                P.mm(banks[z][:, c0:512], nbigi, mk[:, c0:512], False, True, [B_cst], [bankB[z]])
            if un["first"]:
                P.act(qn_b[g % 2], qT[:, GQ * g:GQ * (g + 1)], AF.Copy, q_bufs(g), [B_qn[g % 2]], scale=-1.0)

        def S2a(un):
            g, J, u = un["g"], un["J"], un["u"]
            c0 = col0(g, J)
            z = Zb[u % 2]
            P.act(e_b[u % 2][:, c0:512], banks[z][:, c0:512], AF.Exp, [bankB[z]], [B_e[u % 2]])

        def S2b(un):
            g, J, u = un["g"], un["J"], un["u"]
            c0 = col0(g, J)
            P.act(sp_b[u % 3][:, c0:512], e_b[u % 2][:, c0:512], AF.Ln, [B_e[u % 2]], [B_sp[u % 3]], bias=1.0)

        def S3(un):
            g, J, u = un["g"], un["J"], un["u"]
            kc, vt, kb = kinfo(J)
            pb = Pb[u % 2]
            mk = mask_of(g, J)
            c0 = col0(g, J)
            pv = banks[pb][:, c0:512]
            P.mm(pv, tri_bf, sp_b[u % 3][:, c0:512], True, False, [B_sp[u % 3], B_cst], [bankB[pb]])
            if not un["first"]:
                P.mm(pv, ones_bf, S_t[:, c0:512], False, False, [B_S], [bankB[pb]])
            P.mm(pv, kT[:, kc:kc + 128], qn_b[g % 2][:, c0:512], False, mk is None, k_bufs(J) + [B_qn[g % 2]], [bankB[pb]])
            if mk is not None:
                P.mm(pv, pbigi, mk[:, c0:512], False, True, [B_cst], [bankB[pb]])
            if un["first"]:
                if c0 > 0:
                    P.op("dve", "memset", dict(ap=S_t[:, 0:c0], constant=0.0), writes=[B_S])
                P.copy("dve", S_t[:, c0:512], sp_b[u % 3][:, c0:512], [B_sp[u % 3]], [B_S])
            elif not un["last"]:
                P.tt("dve", S_t[:, c0:512], S_t[:, c0:512], sp_b[u % 3][:, c0:512], ALU.add, [B_sp[u % 3]], [B_S])

        def S4(un):
            g, J, u = un["g"], un["J"], un["u"]
            c0 = col0(g, J)
            pb = Pb[u % 2]
            P.act(w_b[u % 2][:, c0:512], banks[pb][:, c0:512], AF.Exp, [bankB[pb]], [B_w[u % 2]], scale=-1.0)

        def S5(un):
            g, J, u = un["g"], un["J"], un["u"]
            kc, vt, kb = kinfo(J)
            c0 = col0(g, J)
            ob = Ob[g % 2]
            P.mm(banks[ob][:, c0:512], v_sb[:, vt, :], w_b[u % 2][:, c0:512], un["first"], un["last"],
                 [B_v[kb], B_w[u % 2]], [bankB[ob]])
            if un["last"]:
                so = g % 2
                P.copy("dve", ot_st[so], banks[ob][:, :], [bankB[ob]], [B_ot[so]])
                pc, cc0 = (GQ * g) // TOK3, (GQ * g) % TOK3
                store_toks[pc].append(P.dma("pool", inb.ap()[256 * pc:256 * pc + 128, cc0:cc0 + GQ], ot_st[so],
                                            dsem("d_ot%d" % so), reads=[B_ot[so]]))
                if stop_after >= 3 and (GQ * (g + 1)) % TOK3 == 0:
                    P.op("pool", "collective_compute",
                         dict(kind="AllGather", op=ALU.bypass, replica_groups=[list(range(NCORES))],
                              ins=[inb.ap()[256 * pc:256 * (pc + 1), :]],
                              outs=[outb.ap()[NCORES * 256 * pc:NCORES * 256 * (pc + 1), :]]),
                         extra=store_toks[pc], dma=dsem("d_cc"), dma_inc=1)

        stages = [(S1, 0), (S2a, 1), (S4, 3), (S2b, 1), (S3, 2), (S5, 4)]
        nu = len(units)
        for t in range(nu + 4):
            for fnS, dly in stages:
                ui = t - dly
                if 0 <= ui < nu:
                    fnS(units[ui])

    if debug:
        P.dma("sp", dbg["inb"], inb.ap(), dsem("d_dbg"), extra=P.barrier_tokens())

    if stop_after >= 3:
        bar = P.barrier_tokens()

        o3 = [0]

        def c3(n):
            a = o3[0]
            o3[0] += n
            return a

        a_ob = c3(KC * TOK3)
        a_act = c3(FQ * TOK3)
        a_wo = c3(4 * KC * 128)
        a_wg = c3(3 * KC * 128)
        a_wu = c3(3 * KC * 128)
        a_wd = c3(3 * FQ * 128)
        a_sqs = c3(4 * TOK3)
        a_sg = c3(2 * 512)
        assert o3[0] <= ABFN, o3[0]
        obuf = ABF[:, a_ob:a_ob + KC * TOK3].rearrange("p (k n) -> p k n", k=KC)
        actb = ABF[:, a_act:a_act + FQ * TOK3].rearrange("p (f n) -> p f n", f=FQ)
        wo_r = [ABF[:, a_wo + i * KC * 128:a_wo + (i + 1) * KC * 128].rearrange("p (k n) -> p k n", k=KC) for i in range(4)]
        wg_r = [ABF[:, a_wg + i * KC * 128:a_wg + (i + 1) * KC * 128].rearrange("p (k n) -> p k n", k=KC) for i in range(3)]
        wu_r = [ABF[:, a_wu + i * KC * 128:a_wu + (i + 1) * KC * 128].rearrange("p (k n) -> p k n", k=KC) for i in range(3)]
        wd_r = [ABF[:, a_wd + i * FQ * 128:a_wd + (i + 1) * FQ * 128].rearrange("p (f n) -> p f n", f=FQ) for i in range(3)]
        sqs = [ABF[:, a_sqs + i * TOK3:a_sqs + (i + 1) * TOK3] for i in range(4)]
        sg = [ABF[:, a_sg + i * 512:a_sg + (i + 1) * 512] for i in range(2)]

        B_o = [Buf() for _ in range(KC)]
        B_sqs = [Buf() for _ in range(4)]
        B_wo = [Buf() for _ in range(4)]
        B_wg = [Buf() for _ in range(3)]
        B_wu = [Buf() for _ in range(3)]
        B_wd = [Buf() for _ in range(3)]
        B_act = [Buf() for _ in range(FQ)]
        B_sg = [Buf(), Buf()]
        first3 = {e: True for e in P.ENG}

        def x3(eng):
            if first3[eng]:
                first3[eng] = False
                return bar
            return ()

        def TS(T):
            return slice(512 * T, 512 * (T + 1))

        outb3 = outb.ap().rearrange("(q p) t -> p q t", p=128)
        for i in range(4):
            P.op("sp", "dma_start",
                 dict(out=obuf[:, 4 * i:4 * i + 4, :],
                      in_=Dyn((lambda i: lambda ctx: outb3[:, bass.ds(ctx["pid"] * KC + 4 * i, 4), :])(i))),
                 writes=[B_o[4 * i + j] for j in range(4)], dma=dsem("d_ob%d" % i), extra=x3("sp"))

        for kk in range(KC):
            sl = kk % 4
            P.op("act", "activation", dict(out=sqs[sl], in_=obuf[:, kk, :], func=AF.Square),
                 reads=[B_o[kk]], writes=[B_sqs[sl]], extra=x3("act"))
            grp = kk % 2
            for T in range(2):
                bk = 2 * grp + T
                P.op("pe", "matmul", dict(out=banks[bk][:, :], lhsT=ones_bf, rhs=sqs[sl][:, TS(T)], start=(kk < 2), stop=(kk >= KC - 2)),
                     reads=[B_sqs[sl], B_cst], writes=[bankB[bk]], extra=x3("pe"))
        for bk in range(4):
            P.act(banks[6][:, :], banks[bk][:, :], AF.Ln, [bankB[bk]], [bankB[6]], bias=EPS, scale=1.0 / 1024)
            P.act(banks[bk][:, :], banks[6][:, :], AF.Exp, [bankB[6]], [bankB[bk]], scale=-0.5)
        for kk in range(KC):
            grp = kk % 2
            for T in range(2):
                bk = 2 * grp + T
                P.op("dve", "scalar_tensor_tensor",
                     dict(out=obuf[:, kk, TS(T)], in0=obuf[:, kk, TS(T)], scalar=gout(kk), in1=banks[bk][:, :], op0=ALU.mult, op1=ALU.mult),
                     reads=[bankB[bk], B_prm], writes=[B_o[kk]], extra=x3("dve"))
        acc3 = [0]
        for c in range(KC):
            sl = c % 4
            P.op("pool", "dma_start", dict(out=wo_r[sl], in_=wout[c].rearrange("p (k n) -> p k n", k=KC)),
                 writes=[B_wo[sl]], dma=dsem("d_wo%d" % sl), extra=x3("pool"))
            for T in range(2):
                a = 4 + acc3[0] % 2
                acc3[0] += 1
                for kk in range(KC):
                    P.mm(banks[a][:, :], wo_r[sl][:, kk, :], obuf[:, kk, TS(T)], kk == 0, kk == KC - 1, [B_wo[sl], B_o[kk]], [bankB[a]])
                P.tt("dve", hbuf[:, c, TS(T)], banks[a][:, :], hbuf[:, c, TS(T)], ALU.add, [bankB[a]], [B_h[c]])
        for k in range(KC):
            sl = k % 4
            P.act(sqs[sl], hbuf[:, k, :], AF.Square, [B_h[k]], [B_sqs[sl]])
            for T in range(2):
                P.mm(banks[T][:, :], ones_bf, sqs[sl][:, TS(T)], k == 0, k == KC - 1, [B_sqs[sl], B_cst], [bankB[T]])
        for T in range(2):
            P.act(banks[6][:, :], banks[T][:, :], AF.Ln, [bankB[T]], [bankB[6]], bias=EPS, scale=1.0 / D)
            P.act(banks[T][:, :], banks[6][:, :], AF.Exp, [bankB[6]], [bankB[T]], scale=-0.5)
        for k in range(KC):
            for T in range(2):
                P.stt(obuf[:, k, TS(T)], hbuf[:, k, TS(T)], gffn(k), banks[T][:, :], ALU.mult, ALU.mult,
                      [bankB[T], B_h[k], B_prm], [B_o[k]])
        gi_rr, d_rr, wgi, wdi = [0], [0], [0], [0]
        for qq in range(NQ):
            for fl in range(FQ):
                f = qq * FQ + fl
                sl = wgi[0] % 3
                wgi[0] += 1
                P.dma("pool", wg_r[sl], wg[f].rearrange("p (k n) -> p k n", k=KC), dsem("d_wg%d" % sl), writes=[B_wg[sl]])
                P.dma("pool", wu_r[sl], wu[f].rearrange("p (k n) -> p k n", k=KC), dsem("d_wu%d" % sl), writes=[B_wu[sl]])
                for T in range(2):
                    gb = 2 + gi_rr[0] % 2
                    ub = 4 + gi_rr[0] % 2
                    sgi = gi_rr[0] % 2
                    gi_rr[0] += 1
                    for k in range(KC):
                        P.mm(banks[gb][:, :], wg_r[sl][:, k, :], obuf[:, k, TS(T)], k == 0, k == KC - 1, [B_wg[sl], B_o[k]], [bankB[gb]])
                    for k in range(KC):
                        P.mm(banks[ub][:, :], wu_r[sl][:, k, :], obuf[:, k, TS(T)], k == 0, k == KC - 1, [B_wu[sl], B_o[k]], [bankB[ub]])
                    P.act(sg[sgi], banks[gb][:, :], AF.Silu, [bankB[gb]], [B_sg[sgi]])
                    P.tt("dve", actb[:, fl, TS(T)], banks[ub][:, :], sg[sgi], ALU.mult, [bankB[ub], B_sg[sgi]], [B_act[fl]])
            for c in range(KC):
                sl = wdi[0] % 3
                wdi[0] += 1
                P.dma("pool", wd_r[sl], wd[c].rearrange("p (f n) -> p f n", f=NF)[:, qq * FQ:(qq + 1) * FQ, :],
                      dsem("d_wd%d" % sl), writes=[B_wd[sl]])
                for T in range(2):
                    db = 6 + d_rr[0] % 2
                    d_rr[0] += 1
                    for fl in range(FQ):
                        P.mm(banks[db][:, :], wd_r[sl][:, fl, :], actb[:, fl, TS(T)], fl == 0, fl == FQ - 1, [B_wd[sl], B_act[fl]], [bankB[db]])
                    P.tt("dve", hbuf[:, c, TS(T)], banks[db][:, :], hbuf[:, c, TS(T)], ALU.add, [bankB[db]], [B_h[c]])
        y3 = y.rearrange("(k p) t -> p k t", p=128)
        for i in range(8):
            P.dma("sp", y3[:, 2 * i:2 * i + 2, :], hbuf[:, 2 * i:2 * i + 2, :], dsem("d_y"), reads=[B_h[2 * i], B_h[2 * i + 1]])

    P.op("sp", None, extra=P.barrier_tokens())

    all_sems = ["e_" + e for e in P.ENG] + dma_sem_names
    sems = {n: es.enter_context(nc.semaphore(n)) for n in all_sems}
    remap = {e: sorted(P.needed[e]) for e in P.ENG}

    def sem_val(s, v):
        if s.startswith("e_"):
            return bisect.bisect_right(remap[s[2:]], v)
        return v

    block = es.enter_context(nc.Block())

    def lower(engname, eng):
        waited = {}
        ctx = {}
        if engname == "sp":
            ctx["pid"] = eng.partition_id()
        needset = P.needed[engname]
        for waits, meth, kw, sig in P.q[engname]:
            for s, v in waits.items():
                fv = sem_val(s, v)
                if waited.get(s, 0) >= fv:
                    continue
                waited[s] = fv
                eng.wait_ge(sems[s], fv)
            if meth is None:
                continue
            kw2 = {k: (v.f(ctx) if isinstance(v, Dyn) else v) for k, v in kw.items()}
            ins = getattr(eng, meth)(**kw2)
            if sig[0] == "dma":
                ins.then_inc(sems[sig[1]], sig[2])
            elif sig[1] in needset:
                ins.then_inc(sems["e_" + engname], 1)

    @block.tensor
    def _(eng):
        lower("pe", eng)

    @block.scalar
    def _(eng):
        lower("act", eng)

    @block.vector
    def _(eng):
        lower("dve", eng)

    @block.gpsimd
    def _(eng):
        lower("pool", eng)

    @block.sync
    def _(eng):
        lower("sp", eng)

    es.close()
    return nc


def prep_inputs(x, meta_tokens, g_mix, w_in, g_q, g_k, conv_w, g_attn_out, g_conv_out, w_out, g_ffn,
                w_gate, w_up, w_down):
    f = np.float32
    x = np.asarray(x, f)
    hT = np.ascontiguousarray(np.concatenate([np.asarray(meta_tokens, f), x[0]], axis=0).T)
    w_in0 = np.asarray(w_in, f)[0]
    w_out0 = np.asarray(w_out, f)[0]
    rows = []
    for r in range(8):
        rows.append(w_out0[r * 128:(r + 1) * 128])
        rows.append(w_out0[1024 + r * 128:1024 + (r + 1) * 128])
    Wg_ = np.concatenate(rows, axis=0)
    wout = np.ascontiguousarray(Wg_.reshape(KC, 128, KC, 128).transpose(2, 1, 0, 3)).reshape(KC, 128, KC * 128)
    wg = np.ascontiguousarray(np.asarray(w_gate, f)[0].reshape(KC, 128, NF, 128).transpose(2, 1, 0, 3)).reshape(NF, 128, KC * 128)
    wu = np.ascontiguousarray(np.asarray(w_up, f)[0].reshape(KC, 128, NF, 128).transpose(2, 1, 0, 3)).reshape(NF, 128, KC * 128)
    wd = np.ascontiguousarray(np.asarray(w_down, f)[0].reshape(NF, 128, KC, 128).transpose(2, 1, 0, 3)).reshape(KC, 128, NF * 128)
    gmix = np.asarray(g_mix, f)[0].reshape(KC, 128).T
    gffn = np.asarray(g_ffn, f)[0].reshape(KC, 128).T
    ga = np.asarray(g_attn_out, f)[0].reshape(8, 128)
    gc = np.asarray(g_conv_out, f)[0].reshape(8, 128)
    gout = np.zeros((128, KC), f)
    for r in range(8):
        gout[:, 2 * r] = ga[r]
        gout[:, 2 * r + 1] = gc[r]
    in_maps = []
    for c in range(NCORES):
        cols = np.concatenate([np.arange(j * 1024 + c * 128, j * 1024 + (c + 1) * 128) for j in range(6)])
        Wc = w_in0[:, cols]
        winc = np.ascontiguousarray(Wc.reshape(KC, 128, 768).transpose(1, 0, 2)).reshape(128, KC * 768)
        prm = np.zeros((128, 64), f)
        prm[:, 0:16] = gmix
        prm[:, 16] = np.asarray(g_q, f)[0]
        prm[:, 17] = np.asarray(g_k, f)[0]
        prm[:, 18:21] = np.asarray(conv_w, f)[0][:, c * 128:(c + 1) * 128].T
        prm[:, 21:37] = gout
        prm[:, 37:53] = gffn
        in_maps.append({"hT": hT, "win": winc, "prm": prm, "wout": wout, "wg": wg, "wu": wu, "wd": wd})
    return in_maps


def kernel(x, meta_tokens, g_mix, w_in, g_q, g_k, conv_w, g_attn_out, g_conv_out, w_out, g_ffn,
           w_gate, w_up, w_down):
    in_maps = prep_inputs(x, meta_tokens, g_mix, w_in, g_q, g_k, conv_w, g_attn_out, g_conv_out, w_out, g_ffn,
                          w_gate, w_up, w_down)
    nc = build_program()
    res = run_bass_kernel_spmd(nc, in_maps, core_ids=list(range(NCORES)))
    out = np.empty((1, SEQ, D), np.float32)
    for c in range(NCORES):
        yc = np.asarray(res.results[c]["y"], dtype=np.float32)
        out[0, c * TOK3:(c + 1) * TOK3, :] = yc.T
    return out
```

```python
import math
import bisect
import contextlib
import numpy as np
import concourse.bass as bass
import concourse.mybir as mybir
from concourse.bass_utils import run_bass_kernel_spmd

F32 = mybir.dt.float32
BF16 = mybir.dt.bfloat16
AF = mybir.ActivationFunctionType
ALU = mybir.AluOpType

NCORES = 8
D = 2048
SEQ = 8192
NMETA = 16
L = SEQ + NMETA
DFF = 5632
NF = DFF // 128
KC = D // 128
EPS = 1e-6
BIG = 30000.0
G1 = 256
NG1 = SEQ // G1
GQ = 512
NGQ = SEQ // GQ
TOK3 = SEQ // NCORES
NQ = 4
FQ = NF // NQ


class Buf:
    __slots__ = ("w", "r")

    def __init__(self):
        self.w = None
        self.r = {}


class Dyn:
    def __init__(self, f):
        self.f = f


class Prog:
    ENG = ("pe", "act", "dve", "pool", "sp")

    def __init__(self):
        self.q = {e: [] for e in self.ENG}
        self.nops = {e: 0 for e in self.ENG}
        self.needed = {e: set() for e in self.ENG}
        self.dma_cnt = {}

    def op(self, eng, meth, kw=None, reads=(), writes=(), dma=None, dma_inc=16, extra=()):
        waits = {}

        def add(tok):
            if tok is None:
                return
            s, v = tok
            if s == "e_pe" and eng == "pe":
                return
            if waits.get(s, 0) < v:
                waits[s] = v

        for b in reads:
            add(b.w)
        for b in writes:
            add(b.w)
            for s, v in b.r.items():
                add((s, v))
        for t in extra:
            add(t)
        for s, v in waits.items():
            if s.startswith("e_"):
                self.needed[s[2:]].add(v)
        tok = None
        if meth is None:
            self.q[eng].append((waits, None, None, None))
        elif dma is not None:
            self.dma_cnt[dma] = self.dma_cnt.get(dma, 0) + dma_inc
            tok = (dma, self.dma_cnt[dma])
            self.q[eng].append((waits, meth, kw, ("dma", dma, dma_inc)))
        else:
            self.nops[eng] += 1
            tok = ("e_" + eng, self.nops[eng])
            self.q[eng].append((waits, meth, kw, ("eng", self.nops[eng])))
        if tok is not None:
            for b in writes:
                b.w = tok
                b.r = {}
            for b in reads:
                s, v = tok
                if b.r.get(s, 0) < v:
                    b.r[s] = v
        return tok

    def barrier_tokens(self):
        return [("e_" + e, self.nops[e]) for e in self.ENG if self.nops[e] > 0] + \
               [(s, v) for s, v in self.dma_cnt.items()]

    def mm(self, out, lhsT, rhs, start, stop, reads, writes):
        return self.op("pe", "matmul", dict(out=out, lhsT=lhsT, rhs=rhs, start=start, stop=stop), reads, writes)

    def act(self, out, in_, func, reads, writes, bias=None, scale=None):
        kw = dict(out=out, in_=in_, func=func)
        if bias is not None:
            kw["bias"] = bias
        if scale is not None:
            kw["scale"] = scale
        return self.op("act", "activation", kw, reads, writes)

    def tt(self, eng, out, in0, in1, op, reads, writes):
        return self.op(eng, "tensor_tensor", dict(out=out, in0=in0, in1=in1, op=op), reads, writes)

    def stt(self, out, in0, scalar, in1, op0, op1, reads, writes):
        return self.op("dve", "scalar_tensor_tensor", dict(out=out, in0=in0, scalar=scalar, in1=in1, op0=op0, op1=op1),
                       reads, writes)

    def ts(self, eng, out, in0, s1, op0, reads, writes, s2=None, op1=None):
        kw = dict(out=out, in0=in0, scalar1=s1, scalar2=s2, op0=op0)
        if op1 is not None:
            kw["op1"] = op1
        return self.op(eng, "tensor_scalar", kw, reads, writes)

    def copy(self, eng, out, in_, reads, writes):
        return self.op(eng, "tensor_copy", dict(out=out, in_=in_), reads, writes)

    def dma(self, eng, out, in_, sem, reads=(), writes=(), extra=()):
        return self.op(eng, "dma_start", dict(out=out, in_=in_), reads, writes, dma=sem, extra=extra)


def build_program(stop_after=3, debug=False):
    nc = bass.Bass("TRN2", target_bir_lowering=False)
    P = Prog()
    es = contextlib.ExitStack()

    hT = nc.dram_tensor("hT", [D, L], F32, kind="ExternalInput").ap()
    win = nc.dram_tensor("win", [128, KC * 768], F32, kind="ExternalInput").ap()
    prm = nc.dram_tensor("prm", [128, 64], F32, kind="ExternalInput").ap()
    wout = nc.dram_tensor("wout", [KC, 128, KC * 128], F32, kind="ExternalInput").ap()
    wg = nc.dram_tensor("wg", [NF, 128, KC * 128], F32, kind="ExternalInput").ap()
    wu = nc.dram_tensor("wu", [NF, 128, KC * 128], F32, kind="ExternalInput").ap()
    wd = nc.dram_tensor("wd", [KC, 128, NF * 128], F32, kind="ExternalInput").ap()
    y = nc.dram_tensor("y", [D, TOK3], F32, kind="ExternalOutput").ap()
    inb = nc.dram_tensor("inb", [NCORES * 256, TOK3], BF16)
    outb = nc.dram_tensor("outb", [NCORES * NCORES * 256, TOK3], BF16)
    dbg = {}
    if debug:
        dbg["qT"] = nc.dram_tensor("d_qT", [128, SEQ], BF16, kind="ExternalOutput").ap()
        dbg["kT"] = nc.dram_tensor("d_kT", [128, L], BF16, kind="ExternalOutput").ap()
        dbg["v"] = nc.dram_tensor("d_v", [128, 65 * 128], BF16, kind="ExternalOutput").ap()
        dbg["inb"] = nc.dram_tensor("d_inb", [NCORES * 256, TOK3], BF16, kind="ExternalOutput").ap()

    hT3 = hT.rearrange("(k p) t -> p k t", p=128)

    def sb(name, shape, dt):
        return es.enter_context(nc.sbuf_tensor(name, shape, dt))

    ABFN = 57600
    AF32 = sb("arena_f32", [128, 16384], F32)
    ABF = sb("arena_bf", [128, ABFN], BF16)
    prm_sb = sb("prm_sb", [128, 64], F32)
    cst = sb("cst", [128, 5 * 128 + 5 * 512], BF16)
    gqs = sb("gqs", [128, 2], F32)
    p3f = sb("p3f", [128, 6 * 512], F32)

    ones_bf = cst[:, 0:128]
    ident_bf = cst[:, 128:256]
    tri_bf = cst[:, 256:384]
    nbigi = cst[:, 384:512]
    pbigi = cst[:, 512:640]
    maskB = [cst[:, 640 + 512 * i: 640 + 512 * (i + 1)] for i in range(5)]

    gmix = lambda k: prm_sb[:, k:k + 1]
    gq_ap = prm_sb[:, 16:17]
    gk_ap = prm_sb[:, 17:18]
    cw = lambda k: prm_sb[:, 18 + k:19 + k]
    gout = lambda k: prm_sb[:, 21 + k:22 + k]
    gffn = lambda k: prm_sb[:, 37 + k:38 + k]

    o = [0]

    def carve(n):
        a = o[0]
        o[0] += n
        return a

    o_win = carve(KC * 768)
    o_qT = carve(SEQ)
    o_kT = carve(8224)
    o_v = carve(65 * 128)
    o_nbuf = carve(2 * KC * G1)
    o_sq = carve(KC * G1)
    o_qsq = carve(G1)
    o_ksq = carve(G1)
    o_vT = carve(G1)
    o_oc = carve(2 * G1)
    o_sp = carve(3 * 512)
    o_w = carve(2 * 512)
    o_S = carve(2 * 512)
    o_qn = carve(2 * 512)
    o_ot = carve(2 * 512)
    assert o[0] <= ABFN, o[0]

    win_sb = ABF[:, o_win:o_win + KC * 768].rearrange("p (k n) -> p k n", k=KC)
    qT = ABF[:, o_qT:o_qT + SEQ]
    kT = ABF[:, o_kT:o_kT + L]
    v_flat = ABF[:, o_v:o_v + 65 * 128]
    v_sb = v_flat.rearrange("p (t d) -> p t d", d=128)
    nbuf = [ABF[:, o_nbuf + s * KC * G1: o_nbuf + (s + 1) * KC * G1].rearrange("p (k n) -> p k n", k=KC)
            for s in range(2)]
    sqb = [ABF[:, o_sq + i * G1:o_sq + (i + 1) * G1] for i in range(KC)]
    qsq = ABF[:, o_qsq:o_qsq + G1]
    ksq = ABF[:, o_ksq:o_ksq + G1]
    vT_bf = ABF[:, o_vT:o_vT + G1]
    oc_st = [ABF[:, o_oc + i * G1:o_oc + (i + 1) * G1] for i in range(2)]
    sp_b = [ABF[:, o_sp + i * 512:o_sp + (i + 1) * 512] for i in range(3)]
    w_b = [ABF[:, o_w + i * 512:o_w + (i + 1) * 512] for i in range(2)]
    S_b = [ABF[:, o_S + i * 512:o_S + (i + 1) * 512] for i in range(2)]
    qn_b = [ABF[:, o_qn + i * 512:o_qn + (i + 1) * 512] for i in range(2)]
    ot_st = [ABF[:, o_ot + i * 512:o_ot + (i + 1) * 512] for i in range(2)]

    NXS = 3
    xbuf = [AF32[:, s * KC * G1:(s + 1) * KC * G1].rearrange("p (k n) -> p k n", k=KC) for s in range(NXS)]
    fo = [NXS * KC * G1]

    def fcarve(n):
        a = fo[0]
        fo[0] += n
        return AF32[:, a:a + n]

    lnv = [fcarve(G1) for _ in range(2)]
    lnq = fcarve(G1)
    lnk = fcarve(G1)
    qr = fcarve(G1)
    kr = fcarve(G1)
    b_sb = fcarve(G1)
    c_sb = fcarve(G1)
    cu = fcarve(G1 + 8)
    y1 = fcarve(G1)
    y2 = fcarve(G1)
    assert fo[0] <= 16384
    e_t = sb("e_t", [128, 2 * 512], F32)
    e_b = [e_t[:, 0:512], e_t[:, 512:1024]]

    dma_sem_names = []

    def dsem(name):
        if name not in dma_sem_names:
            dma_sem_names.append(name)
        return name

    banks = [es.enter_context(nc.psum_tensor("bank%d" % i, [128, 512], F32)) for i in range(8)]
    bankB = [Buf() for _ in range(8)]

    B_prm = Buf()
    P.dma("sp", prm_sb[:], prm, dsem("d_prm"), writes=[B_prm])
    B_cst = Buf()
    P.op("pool", "memset", dict(ap=cst[:, 0:128], constant=1.0), writes=[B_cst])
    P.op("pool", "memset", dict(ap=cst[:, 640:640 + 2560], constant=1.0), writes=[B_cst])
    P.op("pool", "affine_select", dict(out=ident_bf, in_=ones_bf, pattern=[[1, 128]], compare_op=ALU.is_equal,
                                       fill=0.0, base=0, channel_multiplier=-1), writes=[B_cst])
    P.op("pool", "affine_select", dict(out=tri_bf, in_=ones_bf, pattern=[[-1, 128]], compare_op=ALU.is_ge,
                                       fill=0.0, base=0, channel_multiplier=1), writes=[B_cst])
    P.ts("pool", nbigi, ident_bf, -BIG, ALU.mult, [], [B_cst], s2=1.0, op1=ALU.mult)
    P.ts("pool", pbigi, ident_bf, BIG, ALU.mult, [], [B_cst], s2=1.0, op1=ALU.mult)
    for i in range(4):
        P.op("pool", "affine_select", dict(out=maskB[i], in_=maskB[i], pattern=[[-1, 512]], compare_op=ALU.is_ge,
                                           fill=0.0, base=128 * i, channel_multiplier=1), writes=[B_cst])
    P.op("pool", "affine_select", dict(out=maskB[4], in_=maskB[4], pattern=[[0, 512]], compare_op=ALU.is_ge,
                                       fill=0.0, base=-NMETA, channel_multiplier=1), writes=[B_cst])
    B_gqs = Buf()
    P.ts("dve", gqs[:, 0:1], gq_ap, 1.0 / math.sqrt(128.0), ALU.mult, [B_prm], [B_gqs])
    B_cu = Buf()
    P.op("dve", "memset", dict(ap=cu[:, 0:G1 + 8], constant=0.0), writes=[B_cu])

    B_win = [Buf() for _ in range(4)]
    win3 = win.rearrange("p (k n) -> p k n", k=KC)
    for i in range(4):
        P.dma("pool", win_sb[:, 4 * i:4 * i + 4, :], win3[:, 4 * i:4 * i + 4, :], dsem("d_win%d" % i), writes=[B_win[i]])

    B_x = [[Buf() for _ in range(4)] for _ in range(NXS)]
    B_sq = [Buf() for _ in range(KC)]
    B_n = [[Buf() for _ in range(KC)] for _ in range(2)]
    B_lnv = [Buf(), Buf()]
    B_lnq, B_lnk, B_qr, B_kr, B_qsq, B_ksq, B_vT, B_b, B_c, B_y1, B_y2 = [Buf() for _ in range(11)]
    B_oc = [Buf(), Buf()]
    NGRP = NG1 + 1
    B_qT = [Buf() for _ in range(NGRP)]
    B_kT = [Buf() for _ in range(NGRP)]
    B_v = [Buf() for _ in range(NGRP)]
    P.op("pool", "memset", dict(ap=v_sb[:, 0, :], constant=0.0), writes=[B_v[0]])
    acc_rr = [0]
    store_toks = [[] for _ in range(NCORES)]

    def grp_info(gi):
        if gi == 0:
            return NMETA, 0
        return G1, NMETA + G1 * (gi - 1)

    def dmaX(gi):
        N, c0 = grp_info(gi)
        s = gi % NXS
        for i in range(4):
            P.dma("sp", xbuf[s][:, 4 * i:4 * i + 4, 0:N], hT3[:, 4 * i:4 * i + 4, c0:c0 + N], dsem("d_x%d_%d" % (s, i)),
                  writes=[B_x[s][i]])

    def squares(gi):
        N, c0 = grp_info(gi)
        s = gi % NXS
        for k in range(KC):
            P.act(sqb[k][:, 0:N], xbuf[s][:, k, 0:N], AF.Square, [B_x[s][k // 4]], [B_sq[k]])

    def stageA2(gi):
        N, c0 = grp_info(gi)
        s = gi % NXS
        n = gi % 2
        rb = gi % 2
        for k in range(KC):
            P.mm(banks[rb][:, 0:N], ones_bf, sqb[k][:, 0:N], k == 0, k == KC - 1, [B_sq[k], B_cst], [bankB[rb]])
        P.act(lnv[n][:, 0:N], banks[rb][:, 0:N], AF.Ln, [bankB[rb]], [B_lnv[n]], bias=EPS, scale=1.0 / D)
        P.act(banks[rb][:, 0:N], lnv[n][:, 0:N], AF.Exp, [B_lnv[n]], [bankB[rb]], scale=-0.5)
        for k in range(KC):
            P.stt(nbuf[n][:, k, 0:N], xbuf[s][:, k, 0:N], gmix(k), banks[rb][:, 0:N], ALU.mult, ALU.mult,
                  [B_x[s][k // 4], bankB[rb], B_prm], [B_n[n][k]])

    pending = []

    def flush_pending():
        while pending:
            pending.pop(0)()

    def stageB(gi, js):
        N, c0 = grp_info(gi)
        s = gi % 2
        is_meta = gi == 0
        xc0 = c0 - NMETA
        for j in js:
            if is_meta and j in (0, 3):
                continue
            a = 2 + acc_rr[0] % 3
            acc_rr[0] += 1
            for k in range(KC):
                P.mm(banks[a][:, 0:N], win_sb[:, k, j * 128:(j + 1) * 128], nbuf[s][:, k, 0:N], k == 0, k == KC - 1,
                     [B_n[s][k], B_win[k // 4]], [bankB[a]])
            flush_pending()
            accv = banks[a][:, 0:N]
            if j in (0, 1):
                raw, B_raw, sqt, B_sqt, lnt, B_lnt = (qr, B_qr, qsq, B_qsq, lnq, B_lnq) if j == 0 else \
                    (kr, B_kr, ksq, B_ksq, lnk, B_lnk)
                m = 5 + j
                P.act(raw[:, 0:N], accv, AF.Copy, [bankB[a]], [B_raw])
                P.act(sqt[:, 0:N], accv, AF.Square, [bankB[a]], [B_sqt])

                def post(j=j, m=m, raw=raw, B_raw=B_raw, sqt=sqt, B_sqt=B_sqt, lnt=lnt, B_lnt=B_lnt, N=N, gi=gi, c0=c0, xc0=xc0):
                    P.mm(banks[m][:, 0:N], ones_bf, sqt[:, 0:N], True, True, [B_sqt, B_cst], [bankB[m]])
                    P.act(lnt[:, 0:N], banks[m][:, 0:N], AF.Ln, [bankB[m]], [B_lnt], bias=EPS, scale=1.0 / 128)
                    P.act(banks[m][:, 0:N], lnt[:, 0:N], AF.Exp, [B_lnt], [bankB[m]], scale=-0.5)
                    if j == 0:
                        P.stt(qT[:, xc0:xc0 + N], raw[:, 0:N], gqs[:, 0:1], banks[m][:, 0:N], ALU.mult, ALU.mult,
                              [B_raw, bankB[m], B_gqs], [B_qT[gi]])
                    else:
                        P.stt(kT[:, c0:c0 + N], raw[:, 0:N], gk_ap, banks[m][:, 0:N], ALU.mult, ALU.mult,
                              [B_raw, bankB[m], B_prm], [B_kT[gi]])
                pending.append(post)
            elif j == 2:
                P.act(vT_bf[:, 0:N], accv, AF.Copy, [bankB[a]], [B_vT])

                def postv(N=N, gi=gi, xc0=xc0, is_meta=is_meta):
                    m = 7
                    if is_meta:
                        P.mm(banks[m][0:NMETA, 0:128], vT_bf[:, 0:NMETA], ident_bf, True, True, [B_vT, B_cst], [bankB[m]])
                        P.copy("dve", v_sb[0:NMETA, 0, :], banks[m][0:NMETA, 0:128], [bankB[m]], [B_v[gi]])
                    else:
                        nb = N // 128
                        for i in range(nb):
                            P.mm(banks[m][:, 128 * i:128 * (i + 1)], vT_bf[:, 128 * i:128 * (i + 1)], ident_bf, True, True,
                                 [B_vT, B_cst], [bankB[m]])
                        t0 = 1 + xc0 // 128
                        P.copy("dve", v_sb[:, t0:t0 + nb, :], banks[m][:, 0:N].rearrange("p (t d) -> p t d", d=128),
                               [bankB[m]], [B_v[gi]])
                pending.append(postv)
            elif j == 3:
                P.act(b_sb[:, 0:N], accv, AF.Copy, [bankB[a]], [B_b])
            elif j == 4:
                P.act(c_sb[:, 0:N], accv, AF.Copy, [bankB[a]], [B_c])
            else:
                P.tt("dve", cu[:, 2:2 + N], accv, c_sb[:, 0:N], ALU.mult, [bankB[a], B_c], [B_cu])
                if not is_meta:
                    so = (gi - 1) % 2
                    P.ts("dve", y1[:, 0:N], cu[:, 2:2 + N], cw(2), ALU.mult, [B_cu, B_prm], [B_y1])
                    P.stt(y2[:, 0:N], cu[:, 1:1 + N], cw(1), y1[:, 0:N], ALU.mult, ALU.add, [B_cu, B_y1, B_prm], [B_y2])
                    P.stt(y1[:, 0:N], cu[:, 0:N], cw(0), y2[:, 0:N], ALU.mult, ALU.add, [B_cu, B_y2, B_prm], [B_y1])
                    P.tt("dve", oc_st[so][:, 0:N], y1[:, 0:N], b_sb[:, 0:N], ALU.mult, [B_y1, B_b], [B_oc[so]])
                    pc, cc0 = xc0 // TOK3, xc0 % TOK3
                    store_toks[pc].append(P.dma("pool", inb.ap()[256 * pc + 128:256 * pc + 256, cc0:cc0 + N], oc_st[so][:, 0:N],
                                                dsem("d_oc%d" % so), reads=[B_oc[so]]))
                P.copy("dve", cu[:, 0:2], cu[:, N:N + 2], [B_cu], [B_cu])

    for gi in range(NXS):
        dmaX(gi)
    squares(0)
    stageA2(0)
    squares(1)
    for gi in range(NGRP):
        if gi + NXS < NGRP:
            dmaX(gi + NXS)
        stageB(gi, [0, 1, 2])
        if gi + 1 < NGRP:
            stageA2(gi + 1)
        stageB(gi, [3, 4, 5])
        if gi + 2 < NGRP:
            squares(gi + 2)
    flush_pending()
    bar1 = P.barrier_tokens()

    if debug:
        P.dma("sp", dbg["qT"], qT, dsem("d_dbg"), reads=B_qT)
        P.dma("sp", dbg["kT"], kT, dsem("d_dbg"), reads=B_kT)
        P.dma("sp", dbg["v"], v_flat, dsem("d_dbg"), reads=B_v)

    hbuf = AF32[:, 0:KC * TOK3].rearrange("p (k n) -> p k n", k=KC)
    B_h = [Buf() for _ in range(KC)]
    if stop_after >= 3:
        for i in range(8):
            P.op("sp", "dma_start",
                 dict(out=hbuf[:, 2 * i:2 * i + 2, :],
                      in_=Dyn((lambda i: lambda ctx: hT3[:, 2 * i:2 * i + 2, bass.ds(ctx["pid"] * TOK3 + NMETA, TOK3)])(i))),
                 writes=[B_h[2 * i], B_h[2 * i + 1]], dma=dsem("d_h%d" % i), extra=bar1 if i == 0 else ())

    if stop_after >= 2:
        Zb, Pb, Ob = [0, 1], [2, 3], [4, 7]
        B_e = [Buf(), Buf()]
        B_sp = [Buf(), Buf(), Buf()]
        B_w = [Buf(), Buf()]
        B_S = Buf()
        B_qn = [Buf(), Buf()]
        B_ot = [Buf(), Buf()]
        S_t = S_b[0]
        units = []
        for g in range(NGQ):
            tiles = list(range(4 * g + 3, -1, -1)) + [-1]
            for idx, J in enumerate(tiles):
                units.append(dict(g=g, J=J, first=(idx == 0), last=(idx == len(tiles) - 1), u=len(units)))

        def kinfo(J):
            if J < 0:
                return 0, 0, 0
            return NMETA + 128 * J, 1 + J, 1 + (128 * J) // G1

        def k_bufs(J):
            if J < 0:
                return [B_kT[0], B_kT[1]]
            return [B_kT[1 + (128 * J) // G1]]

        def mask_of(g, J):
            if J < 0:
                return maskB[4]
            return maskB[J - 4 * g] if J >= 4 * g else None

        def col0(g, J):
            if J >= 4 * g:
                return 128 * (J - 4 * g)
            return 0

        def q_bufs(g):
            return [B_qT[1 + (GQ * g) // G1 + i] for i in range(GQ // G1)]

        def S1(un):
            g, J, u = un["g"], un["J"], un["u"]
            kc, vt, kb = kinfo(J)
            z = Zb[u % 2]
            mk = mask_of(g, J)
            c0 = col0(g, J)
            qv = qT[:, GQ * g + c0:GQ * (g + 1)]
            P.mm(banks[z][:, c0:512], kT[:, kc:kc + 128], qv, True, mk is None, k_bufs(J) + q_bufs(g), [bankB[z]])
            if mk is not None:
                P.mm(banks[z][:, c0:512], nbigi, mk[:, c0:512], False, True, [B_cst], [bankB[z]])
            if un["first"]:
                P.act(qn_b[g % 2], qT[:, GQ * g:GQ * (g + 1)], AF.Copy, q_bufs(g), [B_qn[g % 2]], scale=-1.0)

        def S2a(un):
            g, J, u = un["g"], un["J"], un["u"]
            c0 = col0(g, J)
            z = Zb[u % 2]
            P.act(e_b[u % 2][:, c0:512], banks[z][:, c0:512], AF.Exp, [bankB[z]], [B_e[u % 2]])

        def S2b(un):
            g, J, u = un["g"], un["J"], un["u"]
            c0 = col0(g, J)
            P.act(sp_b[u % 3][:, c0:512], e_b[u % 2][:, c0:512], AF.Ln, [B_e[u % 2]], [B_sp[u % 3]], bias=1.0)

        def S3(un):
            g, J, u = un["g"], un["J"], un["u"]
            kc, vt, kb = kinfo(J)
            pb = Pb[u % 2]
            mk = mask_of(g, J)
            c0 = col0(g, J)
            pv = banks[pb][:, c0:512]
            P.mm(pv, tri_bf, sp_b[u % 3][:, c0:512], True, False, [B_sp[u % 3], B_cst], [bankB[pb]])
            if not un["first"]:
                P.mm(pv, ones_bf, S_t[:, c0:512], False, False, [B_S], [bankB[pb]])
            P.mm(pv, kT[:, kc:kc + 128], qn_b[g % 2][:, c0:512], False, mk is None, k_bufs(J) + [B_qn[g % 2]], [bankB[pb]])
            if mk is not None:
                P.mm(pv, pbigi, mk[:, c0:512], False, True, [B_cst], [bankB[pb]])
            if un["first"]:
                if c0 > 0:
                    P.op("dve", "memset", dict(ap=S_t[:, 0:c0], constant=0.0), writes=[B_S])
                P.copy("dve", S_t[:, c0:512], sp_b[u % 3][:, c0:512], [B_sp[u % 3]], [B_S])
            elif not un["last"]:
                P.tt("dve", S_t[:, c0:512], S_t[:, c0:512], sp_b[u % 3][:, c0:512], ALU.add, [B_sp[u % 3]], [B_S])

        def S4(un):
            g, J, u = un["g"], un["J"], un["u"]
            c0 = col0(g, J)
            pb = Pb[u % 2]
            P.act(w_b[u % 2][:, c0:512], banks[pb][:, c0:512], AF.Exp, [bankB[pb]], [B_w[u % 2]], scale=-1.0)

        def S5(un):
            g, J, u = un["g"], un["J"], un["u"]
            kc, vt, kb = kinfo(J)
            c0 = col0(g, J)
            ob = Ob[g % 2]
            P.mm(banks[ob][:, c0:512], v_sb[:, vt, :], w_b[u % 2][:, c0:512], un["first"], un["last"],
                 [B_v[kb], B_w[u % 2]], [bankB[ob]])
            if un["last"]:
                so = g % 2
                P.copy("dve", ot_st[so], banks[ob][:, :], [bankB[ob]], [B_ot[so]])
                pc, cc0 = (GQ * g) // TOK3, (GQ * g) % TOK3
                store_toks[pc].append(P.dma("pool", inb.ap()[256 * pc:256 * pc + 128, cc0:cc0 + GQ], ot_st[so],
                                            dsem("d_ot%d" % so), reads=[B_ot[so]]))
                if stop_after >= 3 and (GQ * (g + 1)) % TOK3 == 0:
                    P.op("pool", "collective_compute",
                         dict(kind="AllGather", op=ALU.bypass, replica_groups=[list(range(NCORES))],
                              ins=[inb.ap()[256 * pc:256 * (pc + 1), :]],
                              outs=[outb.ap()[NCORES * 256 * pc:NCORES * 256 * (pc + 1), :]]),
                         extra=store_toks[pc], dma=dsem("d_cc"), dma_inc=1)

        stages = [(S1, 0), (S2a, 1), (S4, 3), (S2b, 1), (S3, 2), (S5, 4)]
        nu = len(units)
        for t in range(nu + 4):
            for fnS, dly in stages:
                ui = t - dly
                if 0 <= ui < nu:
                    fnS(units[ui])

    if debug:
        P.dma("sp", dbg["inb"], inb.ap(), dsem("d_dbg"), extra=P.barrier_tokens())

    if stop_after >= 3:
        bar = P.barrier_tokens()

        o3 = [0]

        def c3(n):
            a = o3[0]
            o3[0] += n
            return a

        a_ob = c3(KC * TOK3)
        a_act = c3(FQ * TOK3)
        a_wo = c3(4 * KC * 128)
        a_wg = c3(3 * KC * 128)
        a_wu = c3(3 * KC * 128)
        a_wd = c3(3 * FQ * 128)
        a_sqs = c3(4 * TOK3)
        a_sg = c3(2 * 512)
        assert o3[0] <= ABFN, o3[0]
        obuf = ABF[:, a_ob:a_ob + KC * TOK3].rearrange("p (k n) -> p k n", k=KC)
        actb = ABF[:, a_act:a_act + FQ * TOK3].rearrange("p (f n) -> p f n", f=FQ)
        wo_r = [ABF[:, a_wo + i * KC * 128:a_wo + (i + 1) * KC * 128].rearrange("p (k n) -> p k n", k=KC) for i in range(4)]
        wg_r = [ABF[:, a_wg + i * KC * 128:a_wg + (i + 1) * KC * 128].rearrange("p (k n) -> p k n", k=KC) for i in range(3)]
        wu_r = [ABF[:, a_wu + i * KC * 128:a_wu + (i + 1) * KC * 128].rearrange("p (k n) -> p k n", k=KC) for i in range(3)]
        wd_r = [ABF[:, a_wd + i * FQ * 128:a_wd + (i + 1) * FQ * 128].rearrange("p (f n) -> p f n", f=FQ) for i in range(3)]
        sqs = [ABF[:, a_sqs + i * TOK3:a_sqs + (i + 1) * TOK3] for i in range(4)]
        sg = [ABF[:, a_sg + i * 512:a_sg + (i + 1) * 512] for i in range(2)]

        B_o = [Buf() for _ in range(KC)]
        B_sqs = [Buf() for _ in range(4)]
        B_wo = [Buf() for _ in range(4)]
        B_wg = [Buf() for _ in range(3)]
        B_wu = [Buf() for _ in range(3)]
        B_wd = [Buf() for _ in range(3)]
        B_act = [Buf() for _ in range(FQ)]
        B_sg = [Buf(), Buf()]
        first3 = {e: True for e in P.ENG}

        def x3(eng):
            if first3[eng]:
                first3[eng] = False
                return bar
            return ()

        def TS(T):
            return slice(512 * T, 512 * (T + 1))

        outb3 = outb.ap().rearrange("(q p) t -> p q t", p=128)
        for i in range(4):
            P.op("sp", "dma_start",
                 dict(out=obuf[:, 4 * i:4 * i + 4, :],
                      in_=Dyn((lambda i: lambda ctx: outb3[:, bass.ds(ctx["pid"] * KC + 4 * i, 4), :])(i))),
                 writes=[B_o[4 * i + j] for j in range(4)], dma=dsem("d_ob%d" % i), extra=x3("sp"))

        rs = [p3f[:, 512 * i:512 * (i + 1)] for i in range(4)]
        tA = [p3f[:, 2048 + 512 * i:2048 + 512 * (i + 1)] for i in range(2)]
        B_rs = [Buf() for _ in range(4)]
        B_tA = [Buf(), Buf()]
        r2s = [e_b[0], e_b[1]]
        B_r2 = [Buf(), Buf()]
        Gs, Us = [rs[0], rs[1]], [rs[2], rs[3]]
        B_Gs, B_Us = [B_rs[0], B_rs[1]], [B_rs[2], B_rs[3]]
        for kk in range(KC):
            sl = kk % 4
            grp = kk % 2
            P.op("act", "activation", dict(out=sqs[sl], in_=obuf[:, kk, :], func=AF.Square),
                 reads=[B_o[kk]], writes=[B_sqs[sl]], extra=x3("act"))
            P.act(obuf[:, kk, :], obuf[:, kk, :], AF.Identity, [B_prm], [B_o[kk]], scale=gout(kk))
            for T in range(2):
                bk = 2 * grp + T
                P.op("pe", "matmul", dict(out=banks[bk][:, :], lhsT=ones_bf, rhs=sqs[sl][:, TS(T)], start=(kk < 2), stop=(kk >= KC - 2)),
                     reads=[B_sqs[sl], B_cst], writes=[bankB[bk]], extra=x3("pe"))
        for bk in range(4):
            P.act(tA[0], banks[bk][:, :], AF.Ln, [bankB[bk]], [B_tA[0]], bias=EPS, scale=1.0 / 1024)
            P.act(rs[bk], tA[0], AF.Exp, [B_tA[0]], [B_rs[bk]], scale=-0.5)
        acc3 = [0]
        pend3 = []
        for c in range(KC):
            sl = c % 4
            P.op("pool", "dma_start", dict(out=wo_r[sl], in_=wout[c].rearrange("p (k n) -> p k n", k=KC)),
                 writes=[B_wo[sl]], dma=dsem("d_wo%d" % sl), extra=x3("pool"))
            for T in range(2):
                ab = 4 + acc3[0] % 2
                cb = 6 + acc3[0] % 2
                ti = acc3[0] % 2
                acc3[0] += 1
                for i, kk in enumerate(range(0, KC, 2)):
                    P.mm(banks[ab][:, :], wo_r[sl][:, kk, :], obuf[:, kk, TS(T)], i == 0, i == KC // 2 - 1, [B_wo[sl], B_o[kk]], [bankB[ab]])
                for i, kk in enumerate(range(1, KC, 2)):
                    P.mm(banks[cb][:, :], wo_r[sl][:, kk, :], obuf[:, kk, TS(T)], i == 0, i == KC // 2 - 1, [B_wo[sl], B_o[kk]], [bankB[cb]])
                P.op("dve", "tensor_tensor", dict(out=tA[ti], in0=banks[ab][:, :], in1=rs[T], op=ALU.mult),
                     reads=[bankB[ab], B_rs[T]], writes=[B_tA[ti]], extra=x3("dve"))
                P.tt("dve", hbuf[:, c, TS(T)], tA[ti], hbuf[:, c, TS(T)], ALU.add, [B_tA[ti]], [B_h[c]])
                P.tt("dve", tA[ti], banks[cb][:, :], rs[2 + T], ALU.mult, [bankB[cb], B_rs[2 + T]], [B_tA[ti]])
                P.tt("dve", hbuf[:, c, TS(T)], tA[ti], hbuf[:, c, TS(T)], ALU.add, [B_tA[ti]], [B_h[c]])
            while pend3:
                pend3.pop(0)()

            def stat(c=c):
                sl2 = c % 4
                P.act(sqs[sl2], hbuf[:, c, :], AF.Square, [B_h[c]], [B_sqs[sl2]])
                for T in range(2):
                    P.mm(banks[T][:, :], ones_bf, sqs[sl2][:, TS(T)], c == 0, c == KC - 1, [B_sqs[sl2], B_cst], [bankB[T]])
            pend3.append(stat)
        while pend3:
            pend3.pop(0)()
        for T in range(2):
            P.act(tA[0], banks[T][:, :], AF.Ln, [bankB[T]], [B_tA[0]], bias=EPS, scale=1.0 / D)
            P.act(r2s[T], tA[0], AF.Exp, [B_tA[0]], [B_r2[T]], scale=-0.5)
        for k in range(KC):
            P.act(obuf[:, k, :], hbuf[:, k, :], AF.Identity, [B_h[k], B_prm], [B_o[k]], scale=gffn(k))
        gi_rr, d_rr, wgi, wdi = [0], [0], [0], [0]
        for qq in range(NQ):
            for fl in range(FQ):
                f = qq * FQ + fl
                sl = wgi[0] % 3
                wgi[0] += 1
                P.dma("pool", wg_r[sl], wg[f].rearrange("p (k n) -> p k n", k=KC), dsem("d_wg%d" % sl), writes=[B_wg[sl]])
                P.dma("pool", wu_r[sl], wu[f].rearrange("p (k n) -> p k n", k=KC), dsem("d_wu%d" % sl), writes=[B_wu[sl]])
                for T in range(2):
                    gb = 2 + gi_rr[0] % 2
                    ub = 4 + gi_rr[0] % 2
                    sgi = gi_rr[0] % 2
                    gi_rr[0] += 1
                    for k in range(KC):
                        P.mm(banks[gb][:, :], wg_r[sl][:, k, :], obuf[:, k, TS(T)], k == 0, k == KC - 1, [B_wg[sl], B_o[k]], [bankB[gb]])
                    for k in range(KC):
                        P.mm(banks[ub][:, :], wu_r[sl][:, k, :], obuf[:, k, TS(T)], k == 0, k == KC - 1, [B_wu[sl], B_o[k]], [bankB[ub]])
                    P.tt("dve", Gs[sgi], banks[gb][:, :], r2s[T], ALU.mult, [bankB[gb], B_r2[T]], [B_Gs[sgi]])
                    P.act(sg[sgi], Gs[sgi], AF.Silu, [B_Gs[sgi]], [B_sg[sgi]])
                    P.tt("dve", Us[sgi], banks[ub][:, :], r2s[T], ALU.mult, [bankB[ub], B_r2[T]], [B_Us[sgi]])
                    P.tt("dve", actb[:, fl, TS(T)], Us[sgi], sg[sgi], ALU.mult, [B_Us[sgi], B_sg[sgi]], [B_act[fl]])
            for c in range(KC):
                sl = wdi[0] % 3
                wdi[0] += 1
                P.dma("pool", wd_r[sl], wd[c].rearrange("p (f n) -> p f n", f=NF)[:, qq * FQ:(qq + 1) * FQ, :],
                      dsem("d_wd%d" % sl), writes=[B_wd[sl]])
                for T in range(2):
                    db = 6 + d_rr[0] % 2
                    d_rr[0] += 1
                    for fl in range(FQ):
                        P.mm(banks[db][:, :], wd_r[sl][:, fl, :], actb[:, fl, TS(T)], fl == 0, fl == FQ - 1, [B_wd[sl], B_act[fl]], [bankB[db]])
                    P.tt("dve", hbuf[:, c, TS(T)], banks[db][:, :], hbuf[:, c, TS(T)], ALU.add, [bankB[db]], [B_h[c]])
        y3 = y.rearrange("(k p) t -> p k t", p=128)
        for i in range(8):
            P.dma("sp", y3[:, 2 * i:2 * i + 2, :], hbuf[:, 2 * i:2 * i + 2, :], dsem("d_y"), reads=[B_h[2 * i], B_h[2 * i + 1]])

    P.op("sp", None, extra=P.barrier_tokens())

    all_sems = ["e_" + e for e in P.ENG] + dma_sem_names
    sems = {n: es.enter_context(nc.semaphore(n)) for n in all_sems}
    remap = {e: sorted(P.needed[e]) for e in P.ENG}

    def sem_val(s, v):
        if s.startswith("e_"):
            return bisect.bisect_right(remap[s[2:]], v)
        return v

    block = es.enter_context(nc.Block())

    def lower(engname, eng):
        waited = {}
        ctx = {}
        if engname == "sp":
            ctx["pid"] = eng.partition_id()
        needset = P.needed[engname]
        for waits, meth, kw, sig in P.q[engname]:
            for s, v in waits.items():
                fv = sem_val(s, v)
                if waited.get(s, 0) >= fv:
                    continue
                waited[s] = fv
                eng.wait_ge(sems[s], fv)
            if meth is None:
                continue
            kw2 = {k: (v.f(ctx) if isinstance(v, Dyn) else v) for k, v in kw.items()}
            ins = getattr(eng, meth)(**kw2)
            if sig[0] == "dma":
                ins.then_inc(sems[sig[1]], sig[2])
            elif sig[1] in needset:
                ins.then_inc(sems["e_" + engname], 1)

    @block.tensor
    def _(eng):
        lower("pe", eng)

    @block.scalar
    def _(eng):
        lower("act", eng)

    @block.vector
    def _(eng):
        lower("dve", eng)

    @block.gpsimd
    def _(eng):
        lower("pool", eng)

    @block.sync
    def _(eng):
        lower("sp", eng)

    es.close()
    return nc


def prep_inputs(x, meta_tokens, g_mix, w_in, g_q, g_k, conv_w, g_attn_out, g_conv_out, w_out, g_ffn,
                w_gate, w_up, w_down):
    f = np.float32
    x = np.asarray(x, f)
    hT = np.ascontiguousarray(np.concatenate([np.asarray(meta_tokens, f), x[0]], axis=0).T)
    w_in0 = np.asarray(w_in, f)[0]
    w_out0 = np.asarray(w_out, f)[0]
    rows = []
    for r in range(8):
        rows.append(w_out0[r * 128:(r + 1) * 128])
        rows.append(w_out0[1024 + r * 128:1024 + (r + 1) * 128])
    Wg_ = np.concatenate(rows, axis=0)
    wout = np.ascontiguousarray(Wg_.reshape(KC, 128, KC, 128).transpose(2, 1, 0, 3)).reshape(KC, 128, KC * 128)
    wg = np.ascontiguousarray(np.asarray(w_gate, f)[0].reshape(KC, 128, NF, 128).transpose(2, 1, 0, 3)).reshape(NF, 128, KC * 128)
    wu = np.ascontiguousarray(np.asarray(w_up, f)[0].reshape(KC, 128, NF, 128).transpose(2, 1, 0, 3)).reshape(NF, 128, KC * 128)
    wd = np.ascontiguousarray(np.asarray(w_down, f)[0].reshape(NF, 128, KC, 128).transpose(2, 1, 0, 3)).reshape(KC, 128, NF * 128)
    gmix = np.asarray(g_mix, f)[0].reshape(KC, 128).T
    gffn = np.asarray(g_ffn, f)[0].reshape(KC, 128).T
    ga = np.asarray(g_attn_out, f)[0].reshape(8, 128)
    gc = np.asarray(g_conv_out, f)[0].reshape(8, 128)
    gout = np.zeros((128, KC), f)
    for r in range(8):
        gout[:, 2 * r] = ga[r]
        gout[:, 2 * r + 1] = gc[r]
    in_maps = []
    for c in range(NCORES):
        cols = np.concatenate([np.arange(j * 1024 + c * 128, j * 1024 + (c + 1) * 128) for j in range(6)])
        Wc = w_in0[:, cols]
        winc = np.ascontiguousarray(Wc.reshape(KC, 128, 768).transpose(1, 0, 2)).reshape(128, KC * 768)
        prm = np.zeros((128, 64), f)
        prm[:, 0:16] = gmix
        prm[:, 16] = np.asarray(g_q, f)[0]
        prm[:, 17] = np.asarray(g_k, f)[0]
        prm[:, 18:21] = np.asarray(conv_w, f)[0][:, c * 128:(c + 1) * 128].T
        prm[:, 21:37] = gout
        prm[:, 37:53] = gffn
        in_maps.append({"hT": hT, "win": winc, "prm": prm, "wout": wout, "wg": wg, "wu": wu, "wd": wd})
    return in_maps


def kernel(x, meta_tokens, g_mix, w_in, g_q, g_k, conv_w, g_attn_out, g_conv_out, w_out, g_ffn,
           w_gate, w_up, w_down):
    in_maps = prep_inputs(x, meta_tokens, g_mix, w_in, g_q, g_k, conv_w, g_attn_out, g_conv_out, w_out, g_ffn,
                          w_gate, w_up, w_down)
    nc = build_program()
    res = run_bass_kernel_spmd(nc, in_maps, core_ids=list(range(NCORES)))
    out = np.empty((1, SEQ, D), np.float32)
    for c in range(NCORES):
        yc = np.asarray(res.results[c]["y"], dtype=np.float32)
        out[0, c * TOK3:(c + 1) * TOK3, :] = yc.T
    return out
```
